# Optimizing a Trainium2 kernel written in Bass

```python
import jax, jax.numpy as jnp
from jax import lax
import numpy as np

D_MODEL = 1024
BATCH = 2
SEQ = 16384
DEPTH = 2

DENSE_HEAD_DIM = 128
N_FOX_HEADS = 4
N_SB_HEADS = 4
FOX_W = N_FOX_HEADS * DENSE_HEAD_DIM
SB_W = N_SB_HEADS * DENSE_HEAD_DIM
EVEN_WIDTH = FOX_W + SB_W
EVEN_SIZES = (FOX_W, FOX_W, FOX_W, N_FOX_HEADS, SB_W, SB_W, SB_W, EVEN_WIDTH)
EVEN_IN = sum(EVEN_SIZES)
DIL_HEAD_DIM = 64
DILATED_PAIRS = ((128, 1), (512, 4), (2048, 16))
N_DIL_GROUPS = len(DILATED_PAIRS)
N_DIL_HEADS = 8
DIL_W = N_DIL_GROUPS * N_DIL_HEADS * DIL_HEAD_DIM
ODD_WIDTH = N_DIL_HEADS * DIL_HEAD_DIM
ODD_SIZES = (DIL_W, DIL_W, DIL_W, ODD_WIDTH)
ODD_IN = sum(ODD_SIZES)
Q_BLOCK = 128
RMS_EPS = 1e-6
N_EVEN = (DEPTH + 1) // 2
N_ODD = DEPTH // 2

kernel_name = "hybrid_fox_stickbreak_dilated_gated"


def rmsnorm(x, g):
    xf = x.astype(jnp.float32)
    y = xf * lax.rsqrt(jnp.mean(xf * xf, axis=-1, keepdims=True) + RMS_EPS)
    return (y * g.astype(jnp.float32)).astype(x.dtype)


def split_points(sizes):
    return np.cumsum(np.array(sizes))[:-1].tolist()


def split_heads(t, n, hd):
    b, s, _ = t.shape
    return t.reshape(b, s, n, hd).transpose(0, 2, 1, 3)


def merge_heads(t):
    b, n, s, hd = t.shape
    return t.transpose(0, 2, 1, 3).reshape(b, s, n * hd)


def alibi_slopes(n):
    return jnp.asarray(2.0 ** (-8.0 * np.arange(1, n + 1) / n), dtype=jnp.float32)


def forgetting_attention(q, k, v, log_f):
    s = q.shape[2]
    q = q * jnp.asarray(DENSE_HEAD_DIM ** -0.5, q.dtype)
    cum = jnp.cumsum(log_f, axis=-1)
    outs = []
    for i in range(s // Q_BLOCK):
        start, end = i * Q_BLOCK, (i + 1) * Q_BLOCK
        qpos = start + jnp.arange(Q_BLOCK)
        causal = jnp.arange(end)[None, :] <= qpos[:, None]
        logits = jnp.einsum('bhqd,bhkd->bhqk', q[:, :, start:end], k[:, :, :end]).astype(jnp.float32)
        logits = logits + (cum[:, :, start:end, None] - cum[:, :, None, :end])
        p = jax.nn.softmax(jnp.where(causal, logits, -jnp.inf), axis=-1)
        outs.append(jnp.einsum('bhqk,bhkd->bhqd', p.astype(v.dtype), v[:, :, :end]))
    return jnp.concatenate(outs, axis=2)


def stick_breaking_attention(q, k, v):
    b, h, s, _ = q.shape
    q = q * jnp.asarray(DENSE_HEAD_DIM ** -0.5, q.dtype)
    c = jnp.arange(Q_BLOCK)
    upper_incl = (c[:, None] >= c[None, :]).astype(jnp.float32)
    outs = []
    for i in range(s // Q_BLOCK):
        start, end = i * Q_BLOCK, (i + 1) * Q_BLOCK
        nb = i + 1
        qpos = start + jnp.arange(Q_BLOCK)
        strict = jnp.arange(end)[None, :] < qpos[:, None]
        z = jnp.einsum('bhqd,bhkd->bhqk', q[:, :, start:end], k[:, :, :end]).astype(jnp.float32)
        log_beta = jax.nn.log_sigmoid(z)
        log_one_minus = jnp.where(strict, log_beta - z, 0.0)
        lob = log_one_minus.reshape(b, h, Q_BLOCK, nb, Q_BLOCK)
        incl = jnp.einsum('bhqnc,cd->bhqnd', lob, upper_incl)
        n_idx = jnp.arange(nb)
        later_blocks = (n_idx[:, None] > n_idx[None, :]).astype(jnp.float32)
        off = jnp.einsum('bhqn,nm->bhqm', jnp.sum(lob, axis=-1), later_blocks)
        later = (incl - lob + off[..., None]).reshape(b, h, Q_BLOCK, end)
        w = jnp.where(strict, jnp.exp(log_beta + later), 0.0)
        outs.append(jnp.einsum('bhqk,bhkd->bhqd', w.astype(v.dtype), v[:, :, :end]))
    return jnp.concatenate(outs, axis=2)


def dilated_group(q, k, v, window, dil, slopes):
    b, h, s, hd = q.shape
    length = s // dil
    blk = min(Q_BLOCK, length)
    nblk = length // blk
    span = window // dil

    def residues(t):
        return t.reshape(b, h, length, dil, hd).transpose(0, 1, 3, 2, 4).reshape(b, h, dil, nblk, blk, hd)

    def with_prev(t):
        prev = jnp.pad(t[:, :, :, :-1], ((0, 0), (0, 0), (0, 0), (1, 0), (0, 0), (0, 0)))
        return jnp.concatenate([prev, t], axis=4)

    qb = residues(q) * jnp.asarray(DIL_HEAD_DIM ** -0.5, q.dtype)
    kw = with_prev(residues(k))
    vw = with_prev(residues(v))
    a = jnp.arange(blk)[:, None]
    c = jnp.arange(2 * blk)[None, :]
    dist_sub = a - c + blk
    key_idx = jnp.arange(nblk)[:, None, None] * blk + c[None] - blk
    valid = (dist_sub >= 0) & (dist_sub <= span) & (key_idx >= 0)
    logits = jnp.einsum('bhrnqd,bhrnkd->bhrnqk', qb, kw).astype(jnp.float32)
    logits = logits - slopes[:, None, None, None, None] * (dist_sub * dil).astype(jnp.float32)
    logits = jnp.where(valid, logits, -jnp.inf)
    m = jnp.max(logits, axis=-1, keepdims=True)
    p = jnp.exp(logits - m)
    den = jnp.sum(p, axis=-1)
    o = jnp.einsum('bhrnqk,bhrnkd->bhrnqd', p.astype(vw.dtype), vw).astype(jnp.float32) / den[..., None]

    def back(t):
        extra = t.shape[5:]
        t = t.reshape(b, h, dil, length, *extra)
        t = jnp.moveaxis(t, 2, 3)
        return t.reshape(b, h, s, *extra)

    return back(o), back(m[..., 0]), back(den)


def dilated_window_attention(q, k, v):
    slopes = alibi_slopes(N_DIL_GROUPS * N_DIL_HEADS).reshape(N_DIL_GROUPS, N_DIL_HEADS)
    maxes, dens, outs = [], [], []
    for g, (window, dil) in enumerate(DILATED_PAIRS):
        o, m, den = dilated_group(q[g], k[g], v[g], window, dil, slopes[g])
        maxes.append(m); dens.append(den); outs.append(o)
    m_all = jnp.stack(maxes)
    den_all = jnp.stack(dens)
    o_all = jnp.stack(outs)
    wts = den_all * jnp.exp(m_all - jnp.max(m_all, axis=0))
    wts = wts / jnp.sum(wts, axis=0)
    return jnp.sum(wts[..., None] * o_all, axis=0).astype(v.dtype)


def even_layer(x, g_norm, w_in, b_f, g_q, g_k, w_out):
    h = rmsnorm(x, g_norm)
    proj = h @ w_in
    fq, fk, fv, f_logit, sq, sk, sv, gate = jnp.split(proj, split_points(EVEN_SIZES), axis=-1)
    log_f = jax.nn.log_sigmoid((f_logit + b_f).astype(jnp.float32)).transpose(0, 2, 1)
    fq = rmsnorm(split_heads(fq, N_FOX_HEADS, DENSE_HEAD_DIM), g_q)
    fk = rmsnorm(split_heads(fk, N_FOX_HEADS, DENSE_HEAD_DIM), g_k)
    fox = forgetting_attention(fq, fk, split_heads(fv, N_FOX_HEADS, DENSE_HEAD_DIM), log_f)
    sb = stick_breaking_attention(split_heads(sq, N_SB_HEADS, DENSE_HEAD_DIM),
                                  split_heads(sk, N_SB_HEADS, DENSE_HEAD_DIM),
                                  split_heads(sv, N_SB_HEADS, DENSE_HEAD_DIM))
    mixed = jnp.concatenate([merge_heads(fox), merge_heads(sb)], axis=-1) * jax.nn.silu(gate)
    return x + mixed @ w_out


def odd_layer(x, g_norm, w_in, g_q, g_k, w_out):
    h = rmsnorm(x, g_norm)
    proj = h @ w_in
    q, k, v, gate = jnp.split(proj, split_points(ODD_SIZES), axis=-1)
    b, s, _ = x.shape

    def groups(t):
        return t.reshape(b, s, N_DIL_GROUPS, N_DIL_HEADS, DIL_HEAD_DIM).transpose(2, 0, 3, 1, 4)

    q = rmsnorm(groups(q), g_q)
    k = rmsnorm(groups(k), g_k)
    att = dilated_window_attention(q, k, groups(v))
    mixed = merge_heads(att) * jax.nn.silu(gate)
    return x + mixed @ w_out


def setup_inputs(seed: int = 0) -> dict:
    key = jax.random.key(seed)
    ks = jax.random.split(key, 13)
    f32 = jnp.float32
    x = jax.random.normal(ks[0], (BATCH, SEQ, D_MODEL), f32)
    even_norm = 1.0 + 0.02 * jax.random.normal(ks[1], (N_EVEN, D_MODEL), f32)
    even_w_in = jax.random.normal(ks[2], (N_EVEN, D_MODEL, EVEN_IN), f32) * D_MODEL ** -0.5
    even_b_f = (jnp.linspace(1.0, 4.0, N_FOX_HEADS, dtype=f32)[None, :]
                + 0.1 * jax.random.normal(ks[3], (N_EVEN, N_FOX_HEADS), f32))
    even_q_gain = 1.0 + 0.02 * jax.random.normal(ks[4], (N_EVEN, DENSE_HEAD_DIM), f32)
    even_k_gain = 1.0 + 0.02 * jax.random.normal(ks[5], (N_EVEN, DENSE_HEAD_DIM), f32)
    even_w_out = jax.random.normal(ks[6], (N_EVEN, EVEN_WIDTH, D_MODEL), f32) * EVEN_WIDTH ** -0.5
    odd_norm = 1.0 + 0.02 * jax.random.normal(ks[7], (N_ODD, D_MODEL), f32)
    odd_w_in = jax.random.normal(ks[8], (N_ODD, D_MODEL, ODD_IN), f32) * D_MODEL ** -0.5
    odd_q_gain = 1.0 + 0.02 * jax.random.normal(ks[9], (N_ODD, DIL_HEAD_DIM), f32)
    odd_k_gain = 1.0 + 0.02 * jax.random.normal(ks[10], (N_ODD, DIL_HEAD_DIM), f32)
    odd_w_out = jax.random.normal(ks[11], (N_ODD, ODD_WIDTH, D_MODEL), f32) * ODD_WIDTH ** -0.5
    return {"x": x, "even_norm": even_norm, "even_w_in": even_w_in, "even_b_f": even_b_f,
            "even_q_gain": even_q_gain, "even_k_gain": even_k_gain, "even_w_out": even_w_out,
            "odd_norm": odd_norm, "odd_w_in": odd_w_in, "odd_q_gain": odd_q_gain,
            "odd_k_gain": odd_k_gain, "odd_w_out": odd_w_out}


def reference(x, even_norm, even_w_in, even_b_f, even_q_gain, even_k_gain, even_w_out,
              odd_norm, odd_w_in, odd_q_gain, odd_k_gain, odd_w_out):
    h = x
    for layer in range(DEPTH):
        i = layer // 2
        if layer % 2 == 0:
            h = even_layer(h, even_norm[i], even_w_in[i], even_b_f[i], even_q_gain[i],
                           even_k_gain[i], even_w_out[i])
        else:
            h = odd_layer(h, odd_norm[i], odd_w_in[i], odd_q_gain[i], odd_k_gain[i], odd_w_out[i])
    return h
```

```python
import numpy as np
import ml_dtypes
import concourse.bass as bass
import concourse.mybir as mybir
from concourse.bass_utils import run_bass_kernel_spmd

F32 = mybir.dt.float32
BF16 = mybir.dt.bfloat16
F16 = mybir.dt.float16
AF = mybir.ActivationFunctionType
ALU = mybir.AluOpType

NEG = -30000.0
EPS = 1e-6
D = 1024


class Sem:
    def __init__(self, h, step=1):
        self.h = h
        self.v = 0
        self.step = step


class _One:
    def __init__(self):
        self.v = None

    def __getitem__(self, i):
        return self.v

    def __setitem__(self, i, val):
        self.v = val


class Prog:
    ENG = ('pe', 'act', 'dve', 'pool', 'sp')

    def __init__(self, nc, sem_handles):
        self.nc = nc
        self.free_sems = list(sem_handles)
        self.q = {k: [] for k in self.ENG}
        self.waited = {k: {} for k in self.ENG}
        self.pool = []
        self.live = []

    def sem(self, step=1):
        if self.pool:
            s = self.pool.pop()
            s.step = step
        else:
            s = Sem(self.free_sems.pop(), step)
        self.live.append(s)
        return s

    def dsem(self):
        return self.sem(16)

    def mark(self):
        return len(self.live)

    def release(self, mark):
        self.pool.extend(self.live[mark:])
        del self.live[mark:]

    def op(self, eng, fn, waits=(), inc=None):
        ws = tuple(w for w in waits if w is not None)
        tok = None
        incs = None
        if inc is not None:
            inc.v += inc.step
            tok = (inc.h, inc.v)
            incs = (inc.h, inc.step)
        self.q[eng].append((fn, ws, incs))
        return tok

    def emit(self):
        nc = self.nc
        with nc.Block() as block:
            def mk(name):
                items = self.q[name]
                waited = self.waited[name]

                def run(e):
                    for fn, ws, incs in items:
                        for (h, v) in ws:
                            key = id(h)
                            if waited.get(key, 0) >= v:
                                continue
                            waited[key] = v
                            e.wait_ge(h, v)
                        ins = fn(e)
                        if incs is not None:
                            ins.then_inc(incs[0], incs[1])
                return run
            block.tensor(mk('pe'))
            block.scalar(mk('act'))
            block.vector(mk('dve'))
            block.gpsimd(mk('pool'))
            block.sync(mk('sp'))
        self.q = {k: [] for k in self.ENG}


def _consts_p1():
    bf = ml_dtypes.bfloat16
    p = np.arange(128)[:, None]
    f = np.arange(512)[None, :]
    c = {}
    c['ident_bf'] = np.eye(128, dtype=np.float32).astype(bf)
    c['ident_h'] = np.eye(128, dtype=np.float16)
    c['ones_bf'] = np.ones((128, 128), np.float32).astype(bf)
    c['ones_h'] = np.ones((128, 128), np.float16)
    c['negones2'] = (-np.ones((2, 128), np.float32)).astype(bf)
    mf = np.zeros((128, 4, 512), np.float32)
    ms = np.zeros((128, 4, 512), np.float32)
    for jj in range(4):
        mf[:, jj, :] = np.where(128 * jj + p <= f, 0.0, NEG)
        ms[:, jj, :] = np.where(128 * jj + p < f, 0.0, NEG)
    c['mask_fox'] = mf.astype(bf)
    c['mask_sb'] = ms.astype(bf)
    j = np.arange(128)[:, None]
    s = np.arange(128)[None, :]
    c['tincl'] = (j >= s).astype(np.float16)
    c['tlow'] = (j < s).astype(np.float16)
    c['onesrow'] = np.ones((1, 512), np.float32)
    return c


def build_p1(S, debug=False, fused=None):
    NG = S // 512
    NB = S // 128
    pre = "a_" if fused is not None else ""
    nc = fused['nc'] if fused is not None else bass.Bass("TRN2", target_bir_lowering=False)

    def dt(name, *a_, **k_):
        return nc.dram_tensor(pre + name, *a_, **k_)
    xT = dt("xT", [D, S], F32, kind="ExternalInput").ap()
    w = dt("w", [D, 1025], F32, kind="ExternalInput").ap()
    gn = dt("gn", [128, 8], F32, kind="ExternalInput").ap()
    vec = dt("vec", [128, 4], F32, kind="ExternalInput").ap()
    cn = _consts_p1()
    cin = {}
    for k, v in cn.items():
        mdt = {np.dtype('float32'): F32, np.dtype('float16'): F16}.get(v.dtype, BF16)
        cin[k] = dt("c_" + k, list(v.shape), mdt, kind="ExternalInput").ap()
    if fused is not None:
        mixT = dt("mixT", [S // 2048, 256, 2048], BF16, kind="Internal").ap()

        def mix_ap(half, G):
            return mixT[G // 4, half * 128:(half + 1) * 128, (G % 4) * 512:(G % 4) * 512 + 512]
    else:
        mixT = dt("mixT", [256, S], BF16, kind="ExternalOutput").ap()

        def mix_ap(half, G):
            return mixT[half * 128:(half + 1) * 128, G * 512:(G + 1) * 512]
    sk = "ExternalOutput" if debug else "Internal"
    scQ = [dt("scQ%d" % i, [128, S], BF16 if i == 0 else F16, kind=sk).ap() for i in range(2)]
    scK = [dt("scK%d" % i, [128, S], BF16 if i == 0 else F16, kind=sk).ap() for i in range(2)]
    scV = [dt("scV%d" % i, [128, NB, 128], BF16 if i == 0 else F16, kind=sk).ap() for i in range(2)]
    scG = dt("scG", [256, S], F32, kind=sk).ap()
    scC = dt("scC", [2, S], BF16, kind=sk).ap()
    scCC = dt("scCC", [128, NB], F32, kind=sk).ap()

    from contextlib import ExitStack
    with ExitStack() as top:
        def sb(name, shape, dtype):
            return top.enter_context(nc.sbuf_tensor(pre + name, shape, dtype))
        if fused is not None:
            P = fused['P']
        else:
            sems = [top.enter_context(nc.semaphore("s%d" % i)) for i in range(48)]
            P = Prog(nc, sems)
        ident_bf = sb("ident_bf", [128, 128], BF16)
        ident_h = sb("ident_h", [128, 128], F16)
        ones_bf = sb("ones_bf", [128, 128], BF16)
        ones_h = sb("ones_h", [128, 128], F16)
        negones2 = sb("negones2", [2, 128], BF16)
        mask_fox = sb("mask_fox", [128, 4, 512], BF16)
        mask_sb = sb("mask_sb", [128, 4, 512], BF16)
        tincl = sb("tincl", [128, 128], F16)
        tlow = sb("tlow", [128, 128], F16)
        onesrow = sb("onesrow", [1, 512], F32)
        gn_sb = sb("gn_sb", [128, 8], F32)
        vec_sb = sb("vec_sb", [128, 4], F32)
        gqs = sb("gqs", [128, 1], F32)
        negb = sb("negb", [128, 1], F32)
        ccol = sb("ccol", [128, NB], F32)
        s_const = P.dsem()
        ctoks = []
        for name, t in (("ident_bf", ident_bf), ("ident_h", ident_h), ("ones_bf", ones_bf),
                        ("ones_h", ones_h), ("negones2", negones2), ("mask_fox", mask_fox),
                        ("mask_sb", mask_sb), ("tincl", tincl), ("tlow", tlow), ("onesrow", onesrow)):
            ctoks.append(P.op('sp', lambda e, t=t, name=name: e.dma_start(out=t[:], in_=cin[name]), inc=s_const))
        ctoks.append(P.op('sp', lambda e: e.dma_start(out=gn_sb[:], in_=gn), inc=s_const))
        t_const = P.op('sp', lambda e: e.dma_start(out=vec_sb[:], in_=vec), inc=s_const)
        s_misc = P.sem()
        s_phase = P.dsem()
        P.op('dve', lambda e: e.tensor_scalar(gqs[:], vec_sb[:, 0:1], float(128 ** -0.5), None, ALU.mult),
             waits=[t_const], inc=s_misc)
        t_misc = P.op('dve', lambda e: e.tensor_scalar(negb[:], vec_sb[:, 2:3], -1.0, None, ALU.mult), inc=s_misc)

        with ExitStack() as ph:
            _mk = P.mark()
            def sbp(name, shape, dtype):
                return ph.enter_context(nc.sbuf_tensor(pre + name, shape, dtype))

            def psp(name, shape, dtype):
                return ph.enter_context(nc.psum_tensor(pre + name, shape, dtype))
            Wb = sbp("Wb", [128, 8, 1025], BF16)
            wst = [sbp("wst%d" % i, [128, 1025], F32) for i in range(2)]
            xf = [sbp("xf%d" % i, [128, 8, 512], F32) for i in range(2)]
            xb = [sbp("xb%d" % i, [128, 8, 512], BF16) for i in range(2)]
            sqb = sbp("sqb", [128, 8, 512], BF16)
            lnv = sbp("lnv", [128, 512], F32)
            rstd = [sbp("rstd%d" % i, [128, 512], F32) for i in range(2)]
            pq = [sbp("pq%d" % i, [128, 512], F32) for i in range(2)]
            sq2 = [sbp("sq2_%d" % i, [128, 512], BF16) for i in range(2)]
            lnv2 = [sbp("lnv2_%d" % i, [128, 512], F32) for i in range(2)]
            rstd2 = [sbp("rstd2_%d" % i, [128, 512], F32) for i in range(2)]
            qk_o = [[sbp("qko%d_%d" % (i, b), [128, 512], BF16) for b in range(2)] for i in range(2)]
            vT_b = sbp("vT_b", [128, 512], BF16)
            v_tok_b = [sbp("vtokb%d" % i, [128, 4, 128], BF16) for i in range(2)]
            sqk_o = [[sbp("sqko%d_%d" % (i, b), [128, 512], F16) for b in range(2)] for i in range(2)]
            vT_h = sbp("vT_h", [128, 512], F16)
            v_tok_h = [sbp("vtokh%d" % i, [128, 4, 128], F16) for i in range(2)]
            g_raw = [sbp("graw%d" % i, [128, 512], F32) for i in range(2)]
            g_e = [sbp("ge%d" % i, [128, 512], F32) for i in range(2)]
            g_o = [[sbp("go%d_%d" % (i, b), [128, 512], F32) for b in range(2)] for i in range(2)]
            fr_u = sbp("fr_u", [1, 512], F32)
            fr_e = sbp("fr_e", [1, 512], F32)
            fr_sp = sbp("fr_sp", [1, 512], F32)
            fr_c = [sbp("fr_c%d" % i, [1, 512], F32) for i in range(2)]
            fr_hi = [sbp("fr_hi%d" % i, [1, 512], BF16) for i in range(2)]
            fr_hi32 = sbp("fr_hi32", [1, 512], F32)
            fr_lo = [sbp("fr_lo%d" % i, [1, 512], BF16) for i in range(2)]
            one11 = sbp("one11", [1, 1], F32)
            zero11 = sbp("zero11", [1, 1], F32)
            SS = psp("SS", [128, 512], F32)
            SS2 = psp("SS2", [128, 512], F32)
            PJ = [psp("PJ%d" % i, [128, 512], F32) for i in range(3)]
            TPb = psp("TPb", [128, 4, 128], BF16)
            TPh = psp("TPh", [128, 4, 128], F16)
            CC = psp("CC", [128, 4], F32)

            t_one = P.op('pool', lambda e: e.memset(one11[:], 1.0), inc=s_misc)
            t_one = P.op('pool', lambda e: e.memset(zero11[:], 0.0), inc=s_misc)
            s_wl = [P.dsem(), P.dsem()]
            s_wc = P.sem()
            w_r = w.rearrange("(c p) n -> p c n", p=128)
            wc_tok = [None, None]
            t_w = None
            for c in range(8):
                b = c % 2
                tl = P.op('sp', lambda e, c=c, b=b: e.dma_start(out=wst[b][:], in_=w_r[:, c, :]),
                          waits=[wc_tok[b]], inc=s_wl[b])
                t_w = P.op('dve', lambda e, c=c, b=b: e.tensor_scalar(Wb[:, c, :], wst[b][:], gn_sb[:, c:c + 1], None, ALU.mult),
                           waits=[tl, t_const], inc=s_wc)
                wc_tok[b] = t_w

            s_xl = [P.dsem(), P.dsem()]
            s_cast = P.sem()
            s_sq = P.sem()
            s_ss = P.sem()
            s_ln = P.sem()
            s_rstd = P.sem()
            s_pj = P.sem()
            s_ev = P.sem()
            s_act = P.sem()
            s_dve = P.sem()
            s_pe2 = P.sem()
            s_st = [P.dsem() for _ in range(12)]
            s_gst = [[P.dsem() for _ in range(2)] for _ in range(2)]
            s_cst = [P.dsem() for _ in range(2)]
            st_tok = {}
            x_r = xT.rearrange("(c p) s -> p c s", p=128)
            cast_tok = [None, None]
            sq_tok = [None, None]
            pe_x_tok = [None, None]
            ss_read_tok = None
            sqb_free_tok = None
            rstd_free = [None, None]
            pj_free = [None, None, None]
            pj_i = 0
            ss2_free = None
            tpb_free = None
            tph_free = None
            cc_free = None
            prev_c_tok = None
            act_prev = None
            dve_prev = None
            scan_prev = None

            for n in range(NG):
                b = n % 2
                cs = slice(n * 512, (n + 1) * 512)
                tl = None
                for hlf in range(2):
                    tl = P.op('sp', lambda e, b=b, hlf=hlf, cs=cs: e.dma_start(
                        out=xf[b][:, 4 * hlf:4 * hlf + 4, :], in_=x_r[:, 4 * hlf:4 * hlf + 4, cs]),
                        waits=[cast_tok[b], sq_tok[b]], inc=s_xl[b])
                cast_tok[b] = P.op('pool', lambda e, b=b: e.tensor_copy(xb[b][:], xf[b][:]),
                                   waits=[tl, pe_x_tok[b]], inc=s_cast)
                sq_tok[b] = P.op('act', lambda e, b=b: e.activation(sqb[:], xf[b][:], AF.Square),
                                 waits=[tl, sqb_free_tok], inc=s_sq)
                for c in range(8):
                    t = P.op('pe', lambda e, c=c: e.matmul(SS[:], ones_bf[:], sqb[:, c, :], start=(c == 0), stop=(c == 7)),
                             waits=[sq_tok[b], ss_read_tok, ctoks[2]] if c == 0 else (), inc=s_ss if c == 7 else None)
                ss_tok = t
                sqb_free_tok = ss_tok
                t = P.op('act', lambda e: e.activation(lnv[:], SS[:], AF.Ln, bias=EPS, scale=1.0 / 1024),
                         waits=[ss_tok, act_prev], inc=s_ln)
                ss_read_tok = t
                rstd_tok = P.op('act', lambda e, b=b: e.activation(rstd[b][:], lnv[:], AF.Exp, scale=-0.5),
                                waits=[t, rstd_free[b]], inc=s_rstd)
                act_prev = rstd_tok

                def proj(cb, M=128):
                    nonlocal pj_i
                    k = pj_i % 3
                    pj_i += 1
                    t = None
                    for c in range(8):
                        t = P.op('pe', lambda e, c=c, k=k, cb=cb, M=M, b=b: e.matmul(
                            PJ[k][0:M, :], Wb[:, c, cb * 128:cb * 128 + M], xb[b][:, c, :], start=(c == 0), stop=(c == 7)),
                            waits=[cast_tok[b], t_w, pj_free[k]] if c == 0 else (), inc=s_pj if c == 7 else None)
                    return k, t

                def evac(k, fn, waits=()):
                    nonlocal dve_prev
                    t = P.op('dve', fn, waits=list(waits) + [rstd_tok, dve_prev], inc=s_ev)
                    pj_free[k] = t
                    dve_prev = t
                    return t

                for i in range(2):
                    k, tp = proj(i)
                    tq = evac(k, lambda e, k=k, i=i, b=b: e.tensor_tensor(pq[i][:], PJ[k][:], rstd[b][:], ALU.mult), [tp])
                    t = P.op('act', lambda e, i=i: e.activation(sq2[i][:], pq[i][:], AF.Square), waits=[tq, act_prev], inc=s_act)
                    act_prev = t
                    t = P.op('pe', lambda e, i=i: e.matmul(SS2[:], ones_bf[:], sq2[i][:], start=True, stop=True),
                             waits=[t, ss2_free], inc=s_pe2)
                    t = P.op('act', lambda e, i=i: e.activation(lnv2[i][:], SS2[:], AF.Ln, bias=EPS, scale=1.0 / 128),
                             waits=[t, act_prev], inc=s_act)
                    ss2_free = t
                    t = P.op('act', lambda e, i=i: e.activation(rstd2[i][:], lnv2[i][:], AF.Exp, scale=-0.5), waits=[t], inc=s_act)
                    act_prev = t
                    gcol = gqs[:, 0:1] if i == 0 else vec_sb[:, 1:2]
                    sl = (0, i, b)
                    t = P.op('dve', lambda e, i=i, b=b, gcol=gcol: e.scalar_tensor_tensor(
                        qk_o[i][b][:], pq[i][:], gcol, rstd2[i][:], ALU.mult, ALU.mult),
                        waits=[t, dve_prev, t_misc, st_tok.get(sl)], inc=s_dve)
                    dve_prev = t
                    dst = (scQ[0] if i == 0 else scK[0])
                    st_tok[sl] = P.op('pool', lambda e, i=i, b=b, dst=dst, cs=cs: e.dma_start(out=dst[:, cs], in_=qk_o[i][b][:]),
                                      waits=[t], inc=s_st[i * 2 + b])
                k, tp = proj(2)
                tv = evac(k, lambda e, k=k, b=b: e.tensor_tensor(vT_b[:], PJ[k][:], rstd[b][:], ALU.mult), [tp])
                t = None
                for i in range(4):
                    t = P.op('pe', lambda e, i=i: e.transpose(TPb[:, i, :], vT_b[:, i * 128:(i + 1) * 128], ident_bf[:]),
                             waits=[tv, tpb_free] if i == 0 else (), inc=s_pe2 if i == 3 else None)
                sl = (1, b)
                t = P.op('dve', lambda e, b=b: e.tensor_copy(v_tok_b[b][:], TPb[:]), waits=[t, dve_prev, st_tok.get(sl)], inc=s_dve)
                dve_prev = t
                tpb_free = t
                st_tok[sl] = P.op('pool', lambda e, b=b, n=n: e.dma_start(out=scV[0][:, 4 * n:4 * n + 4, :], in_=v_tok_b[b][:]),
                                  waits=[t], inc=s_st[4 + b])
                for i in range(2):
                    k, tp = proj(3 + i)
                    sl = (2, i, b)
                    if i == 0:
                        tq = evac(k, lambda e, k=k, i=i, b=b: e.scalar_tensor_tensor(
                            sqk_o[i][b][:], PJ[k][:], float(128 ** -0.5), rstd[b][:], ALU.mult, ALU.mult), [tp, st_tok.get(sl)])
                    else:
                        tq = evac(k, lambda e, k=k, i=i, b=b: e.tensor_tensor(sqk_o[i][b][:], PJ[k][:], rstd[b][:], ALU.mult),
                                  [tp, st_tok.get(sl)])
                    dst = (scQ[1] if i == 0 else scK[1])
                    st_tok[sl] = P.op('pool', lambda e, i=i, b=b, dst=dst, cs=cs: e.dma_start(out=dst[:, cs], in_=sqk_o[i][b][:]),
                                      waits=[tq], inc=s_st[6 + i * 2 + b])
                k, tp = proj(5)
                tv = evac(k, lambda e, k=k, b=b: e.tensor_tensor(vT_h[:], PJ[k][:], rstd[b][:], ALU.mult), [tp])
                t = None
                for i in range(4):
                    t = P.op('pe', lambda e, i=i: e.transpose(TPh[:, i, :], vT_h[:, i * 128:(i + 1) * 128], ident_h[:]),
                             waits=[tv, tph_free] if i == 0 else (), inc=s_pe2 if i == 3 else None)
                sl = (3, b)
                t = P.op('dve', lambda e, b=b: e.tensor_copy(v_tok_h[b][:], TPh[:]), waits=[t, dve_prev, st_tok.get(sl)], inc=s_dve)
                dve_prev = t
                tph_free = t
                st_tok[sl] = P.op('pool', lambda e, b=b, n=n: e.dma_start(out=scV[1][:, 4 * n:4 * n + 4, :], in_=v_tok_h[b][:]),
                                  waits=[t], inc=s_st[10 + b])
                for i in range(2):
                    k, tp = proj(6 + i)
                    tg = evac(k, lambda e, k=k, i=i, b=b: e.tensor_tensor(g_raw[i][:], PJ[k][:], rstd[b][:], ALU.mult), [tp])
                    t = P.op('act', lambda e, i=i: e.activation(g_e[i][:], g_raw[i][:], AF.Exp, scale=-1.0), waits=[tg, act_prev], inc=s_act)
                    act_prev = t
                    t = P.op('dve', lambda e, i=i: e.tensor_scalar(g_e[i][:], g_e[i][:], 1.0, None, ALU.add), waits=[t, dve_prev], inc=s_dve)
                    t = P.op('dve', lambda e, i=i: e.reciprocal(g_e[i][:], g_e[i][:]), waits=[t], inc=s_dve)
                    sl = (4, i, b)
                    t = P.op('dve', lambda e, i=i, b=b: e.tensor_tensor(g_o[i][b][:], g_raw[i][:], g_e[i][:], ALU.mult),
                             waits=[t, st_tok.get(sl)], inc=s_dve)
                    dve_prev = t
                    st_tok[sl] = P.op('sp', lambda e, i=i, b=b, cs=cs: e.dma_start(out=scG[i * 128:(i + 1) * 128, cs], in_=g_o[i][b][:]),
                                      waits=[t], inc=s_gst[i][b])
                k, tp = proj(8, M=1)
                tu = evac(k, lambda e, k=k, b=b: e.tensor_tensor(fr_u[:], PJ[k][0:1, :], rstd[b][0:1, :], ALU.mult), [tp, scan_prev])
                rstd_free[b] = tu
                pe_x_tok[b] = tp
                t = P.op('act', lambda e: e.activation(fr_e[:], fr_u[:], AF.Exp, bias=negb[0:1, 0:1], scale=-1.0),
                         waits=[tu, act_prev, t_misc], inc=s_act)
                t = P.op('act', lambda e: e.activation(fr_sp[:], fr_e[:], AF.Ln, bias=1.0), waits=[t], inc=s_act)
                act_prev = t
                init = zero11[0:1, 0:1] if n == 0 else fr_c[1 - b][0:1, 511:512]
                sl = (5, b)
                t = P.op('dve', lambda e, b=b, init=init: e.tensor_tensor_scan(fr_c[b][:], onesrow[:], fr_sp[:], init, ALU.mult, ALU.add),
                         waits=[t, dve_prev, cc_free, t_one, st_tok.get(sl), ctoks[9]], inc=s_dve)
                tc = t
                t = P.op('dve', lambda e, b=b: e.tensor_copy(fr_hi[b][:], fr_c[b][:]), waits=[t], inc=s_dve)
                t = P.op('dve', lambda e, b=b: e.tensor_copy(fr_hi32[:], fr_hi[b][:]), waits=[t], inc=s_dve)
                t = P.op('dve', lambda e, b=b: e.tensor_tensor(fr_lo[b][:], fr_c[b][:], fr_hi32[:], ALU.subtract), waits=[t], inc=s_dve)
                dve_prev = t
                scan_prev = t
                P.op('pool', lambda e, b=b, cs=cs: e.dma_start(out=scC[0:1, cs], in_=fr_hi[b][:]), waits=[t], inc=s_cst[b])
                st_tok[sl] = P.op('pool', lambda e, b=b, cs=cs: e.dma_start(out=scC[1:2, cs], in_=fr_lo[b][:]), waits=[t], inc=s_cst[b])
                t = None
                for i in range(4):
                    t = P.op('pe', lambda e, i=i, b=b: e.matmul(CC[:, i:i + 1], fr_c[b][0:1, i * 128:(i + 1) * 128], one11[0:1, 0:1],
                                                               start=True, stop=True),
                             waits=[tc] if i == 0 else (), inc=s_pe2 if i == 3 else None)
                t = P.op('dve', lambda e, n=n: e.tensor_copy(ccol[:, 4 * n:4 * n + 4], CC[:]), waits=[t, dve_prev], inc=s_dve)
                dve_prev = t
                cc_free = t
            fin = [v for v in st_tok.values() if v is not None]
            if debug:
                fin.append(P.op('sp', lambda e: e.dma_start(out=scCC, in_=ccol[:]), waits=[dve_prev], inc=s_cst[0]))
            P.op('sp', lambda e: e.dma_start(out=zero11[:], in_=cin['onesrow'][0:1, 0:1]), waits=fin + [dve_prev], inc=s_phase)
            P.emit()
            P.release(_mk)

        tp_phase = (s_phase.h, s_phase.v)
        with ExitStack() as ph:
            _mk = P.mark()
            def sbp(name, shape, dtype):
                return ph.enter_context(nc.sbuf_tensor(pre + name, shape, dtype))

            def psp(name, shape, dtype):
                return ph.enter_context(nc.psum_tensor(pre + name, shape, dtype))
            QT = sbp("fQT", [128, S], BF16)
            KT = sbp("fKT", [128, S], BF16)
            V = sbp("fV", [128, NB, 128], BF16)
            fr = sbp("frame", [2, S], BF16)
            Pt = [sbp("fP%d" % i, [128, 512], BF16) for i in range(3)]
            gt = [sbp("fgt%d" % i, [128, 512], F32) for i in range(2)]
            rden = sbp("rden", [128, 512], F32)
            o32 = sbp("o32", [128, 512], F32)
            mo = [sbp("fmo%d" % i, [128, 512], BF16) for i in range(2)]
            Sb = [psp("fSb%d" % i, [128, 512], F32) for i in range(2)]
            OT = [psp("fOT%d" % i, [128, 512], F32) for i in range(2)]
            DEN = [psp("fDEN%d" % i, [128, 512], F32) for i in range(2)]
            s_ld = P.dsem()
            s_qk = P.sem()
            s_p = P.sem()
            s_pv = P.sem()
            s_fd = P.sem()
            s_gl = [P.dsem(), P.dsem()]
            s_os = [P.dsem(), P.dsem()]
            nq = 4 if S >= 2048 else 1
            tl = None
            for src_ap, dst in ((scQ[0], QT), (scK[0], KT)):
                for q in range(nq):
                    sl = slice(q * (S // nq), (q + 1) * (S // nq))
                    tl = P.op('sp', lambda e, src_ap=src_ap, dst=dst, sl=sl: e.dma_start(out=dst[:, sl], in_=src_ap[:, sl]),
                              waits=[tp_phase], inc=s_ld)
            for q in range(nq):
                sl = slice(q * (NB // nq), (q + 1) * (NB // nq))
                tl = P.op('sp', lambda e, sl=sl: e.dma_start(out=V[:, sl, :], in_=scV[0][:, sl, :]), inc=s_ld)
            t_ld = P.op('sp', lambda e: e.dma_start(out=fr[:], in_=scC), inc=s_ld)
            tiles = [(G, j) for G in range(NG) for j in range(4 * G + 4)]
            NT = len(tiles)
            tok_qk = [None] * NT
            tok_p = [None] * NT
            tok_pv = [None] * NT
            gl_tok = [None, None]
            fd_tok = [None, None]
            os_tok = [None, None]
            gt_free = [None, None]

            def emit_qk(i):
                G, j = tiles[i]
                sbk = i % 2
                gs = slice(G * 512, (G + 1) * 512)
                diag = j >= 4 * G
                P.op('pe', lambda e: e.matmul(Sb[sbk][:], KT[:, j * 128:(j + 1) * 128], QT[:, gs], start=True, stop=False),
                     waits=[t_ld, tok_p[i - 2] if i >= 2 else None])
                t = P.op('pe', lambda e: e.matmul(Sb[sbk][:], negones2[0:2, :], fr[0:2, gs], start=False, stop=(not diag)),
                         inc=None if diag else s_qk)
                if diag:
                    jj = j - 4 * G
                    t = P.op('pe', lambda e: e.matmul(Sb[sbk][:], ident_bf[:], mask_fox[:, jj, :], start=False, stop=True), inc=s_qk)
                tok_qk[i] = t

            def emit_exp(i):
                G, j = tiles[i]
                tok_p[i] = P.op('act', lambda e: e.activation(Pt[i % 3][:], Sb[i % 2][:], AF.Exp, bias=ccol[:, j:j + 1], scale=1.0),
                                waits=[tok_qk[i], tok_pv[i - 3] if i >= 3 else None], inc=s_p)

            def emit_pv(i):
                G, j = tiles[i]
                last = 4 * G + 3
                g2 = G % 2
                P.op('pe', lambda e: e.matmul(OT[g2][:], V[:, j, :], Pt[i % 3][:], start=(j == 0), stop=(j == last)),
                     waits=[tok_p[i], fd_tok[g2] if j == 0 else None])
                tok_pv[i] = P.op('pe', lambda e: e.matmul(DEN[g2][:], ones_bf[:], Pt[i % 3][:], start=(j == 0), stop=(j == last)),
                                 inc=s_pv)
                if j == 0:
                    gs = slice(G * 512, (G + 1) * 512)
                    gl_tok[g2] = P.op('sp', lambda e: e.dma_start(out=gt[g2][:], in_=scG[0:128, gs]), waits=[gt_free[g2]], inc=s_gl[g2])
                if j == last:
                    gs = slice(G * 512, (G + 1) * 512)
                    t = P.op('dve', lambda e: e.reciprocal(rden[:], DEN[g2][:]), waits=[tok_pv[i], fd_tok[1 - g2]], inc=s_fd)
                    t = P.op('dve', lambda e: e.tensor_tensor(o32[:], OT[g2][:], rden[:], ALU.mult), waits=[t], inc=s_fd)
                    fd_tok[g2] = t
                    t = P.op('dve', lambda e: e.tensor_tensor(mo[g2][:], o32[:], gt[g2][:], ALU.mult), waits=[t, gl_tok[g2], os_tok[g2]], inc=s_fd)
                    gt_free[g2] = t
                    fd_tok[g2] = t
                    os_tok[g2] = P.op('pool', lambda e: e.dma_start(out=mix_ap(0, G), in_=mo[g2][:]), waits=[t], inc=s_os[g2])

            emit_qk(0)
            if NT > 1:
                emit_qk(1)
            for i in range(NT):
                emit_exp(i)
                emit_pv(i)
                if i + 2 < NT:
                    emit_qk(i + 2)
            P.op('sp', lambda e: e.dma_start(out=rden[0:1, 0:1], in_=cin['onesrow'][0:1, 0:1]),
                 waits=[os_tok[0], os_tok[1], fd_tok[0], fd_tok[1]], inc=s_phase)
            P.emit()
            P.release(_mk)

        tp_phase = (s_phase.h, s_phase.v)
        with ExitStack() as ph:
            _mk = P.mark()
            def sbp(name, shape, dtype):
                return ph.enter_context(nc.sbuf_tensor(pre + name, shape, dtype))

            def psp(name, shape, dtype):
                return ph.enter_context(nc.psum_tensor(pre + name, shape, dtype))
            QT = sbp("sQT", [128, S], F16)
            KT = sbp("sKT", [128, S], F16)
            V = sbp("sV", [128, NB, 128], F16)
            e32 = [sbp("e32_%d" % i, [128, 512], F32) for i in range(3)]
            Lh = [sbp("Lh%d" % i, [128, 512], F16) for i in range(3)]
            x32 = [sbp("x32_%d" % i, [128, 512], F32) for i in range(2)]
            wh = [sbp("wh%d" % i, [128, 512], F16) for i in range(3)]
            gt = [sbp("sgt%d" % i, [128, 512], F32) for i in range(2)]
            mo = [sbp("smo%d" % i, [128, 512], BF16) for i in range(2)]
            Z = [psp("sZ%d" % i, [128, 512], F32) for i in range(2)]
            CH = [psp("sCH%d" % i, [128, 512], F32) for i in range(2)]
            OT = [psp("sOT%d" % i, [128, 512], F32) for i in range(2)]
            s_ld = P.dsem()
            s_qk = P.sem()
            s_a1 = P.sem()
            s_a2 = P.sem()
            s_c1 = P.sem()
            s_a3 = P.sem()
            s_w = P.sem()
            s_pv = P.sem()
            s_tl = P.sem()
            s_fd = P.sem()
            s_gl = [P.dsem(), P.dsem()]
            s_os = [P.dsem(), P.dsem()]
            nq = 4 if S >= 2048 else 1
            for src_ap, dst in ((scQ[1], QT), (scK[1], KT)):
                for q in range(nq):
                    sl = slice(q * (S // nq), (q + 1) * (S // nq))
                    P.op('sp', lambda e, src_ap=src_ap, dst=dst, sl=sl: e.dma_start(out=dst[:, sl], in_=src_ap[:, sl]),
                         waits=[tp_phase], inc=s_ld)
            t_ld = None
            for q in range(nq):
                sl = slice(q * (NB // nq), (q + 1) * (NB // nq))
                t_ld = P.op('sp', lambda e, sl=sl: e.dma_start(out=V[:, sl, :], in_=scV[1][:, sl, :]), inc=s_ld)
            tiles = [(G, 4 * G + 3 - m, m) for G in range(NG) for m in range(4 * G + 4)]
            NT = len(tiles)
            tok_qk = [None] * NT
            tok_a1 = [None] * NT
            tok_a2 = [None] * NT
            tok_c1 = [None] * NT
            tok_a3 = [None] * NT
            tok_w = [None] * NT
            tok_pv = [None] * NT
            tok_tl = [None] * NT
            gl_tok = [None, None]
            fd_tok = [None, None]
            os_tok = [None, None]
            gt_free = [None, None]

            def s_qk_emit(i):
                G, j, m = tiles[i]
                gs = slice(G * 512, (G + 1) * 512)
                diag = j >= 4 * G
                t = P.op('pe', lambda e: e.matmul(Z[i % 2][:], KT[:, j * 128:(j + 1) * 128], QT[:, gs], start=True, stop=(not diag)),
                         waits=[t_ld, tok_a1[i - 2] if i >= 2 else None], inc=None if diag else s_qk)
                if diag:
                    jj = j - 4 * G
                    t = P.op('pe', lambda e: e.matmul(Z[i % 2][:], ident_bf[:], mask_sb[:, jj, :], start=False, stop=True), inc=s_qk)
                tok_qk[i] = t

            def s_a12(i):
                tok_a1[i] = P.op('act', lambda e: e.activation(e32[i % 3][:], Z[i % 2][:], AF.Exp),
                                 waits=[tok_qk[i], tok_w[i - 3] if i >= 3 else None], inc=s_a1)
                tok_a2[i] = P.op('act', lambda e: e.activation(Lh[i % 3][:], e32[i % 3][:], AF.Ln, bias=1.0),
                                 waits=[tok_a1[i], tok_tl[i - 3] if i >= 3 else None], inc=s_a2)

            def s_chain(i):
                G, j, m = tiles[i]
                own = CH[m % 2]
                oth = CH[(m + 1) % 2]
                first = (m == 0)
                prev_a3 = tok_a3[i - 1] if i >= 1 else None
                tok_c1[i] = P.op('pe', lambda e: e.matmul(own[:], tincl[:], Lh[i % 3][:], start=first, stop=True, skip_group_check=True),
                                 waits=[tok_a2[i], prev_a3 if first else None], inc=s_c1)
                if i >= 1 and tiles[i - 1][2] != 4 * tiles[i - 1][0] + 3:
                    tok_tl[i - 1] = P.op('pe', lambda e: e.matmul(oth[:], tlow[:], Lh[(i - 1) % 3][:], start=False, stop=True,
                                                                  skip_group_check=True),
                                         waits=[tok_a3[i - 1]], inc=s_tl)
                if m != 4 * G + 3:
                    tok_tl[i] = P.op('pe', lambda e: e.matmul(oth[:], ones_h[:], Lh[i % 3][:], start=first, stop=True, skip_group_check=True),
                                     waits=[prev_a3 if first else None], inc=s_tl)
                else:
                    tok_tl[i] = tok_c1[i]

            def s_a3w(i):
                G, j, m = tiles[i]
                own = CH[m % 2]
                tok_a3[i] = P.op('act', lambda e: e.activation(x32[i % 2][:], own[:], AF.Exp, scale=-1.0),
                                 waits=[tok_c1[i], tok_w[i - 2] if i >= 2 else None], inc=s_a3)
                tok_w[i] = P.op('dve', lambda e: e.tensor_tensor(wh[i % 3][:], e32[i % 3][:], x32[i % 2][:], ALU.mult),
                                waits=[tok_a3[i], tok_pv[i - 3] if i >= 3 else None], inc=s_w)

            def s_pv_emit(i):
                G, j, m = tiles[i]
                last = 4 * G + 3
                g2 = G % 2
                tok_pv[i] = P.op('pe', lambda e: e.matmul(OT[g2][:], V[:, j, :], wh[i % 3][:], start=(m == 0), stop=(m == last)),
                                 waits=[tok_w[i], fd_tok[g2] if m == 0 else None], inc=s_pv)
                gs = slice(G * 512, (G + 1) * 512)
                if m == 0:
                    gl_tok[g2] = P.op('sp', lambda e: e.dma_start(out=gt[g2][:], in_=scG[128:256, gs]), waits=[gt_free[g2]], inc=s_gl[g2])
                if m == last:
                    t = P.op('dve', lambda e: e.tensor_tensor(mo[g2][:], OT[g2][:], gt[g2][:], ALU.mult),
                             waits=[tok_pv[i], gl_tok[g2], os_tok[g2]], inc=s_fd)
                    fd_tok[g2] = t
                    gt_free[g2] = t
                    os_tok[g2] = P.op('pool', lambda e: e.dma_start(out=mix_ap(1, G), in_=mo[g2][:]), waits=[t], inc=s_os[g2])

            s_qk_emit(0)
            if NT > 1:
                s_qk_emit(1)
            for i in range(NT + 2):
                if i < NT:
                    s_a12(i)
                if 1 <= i <= NT:
                    s_a3w(i - 1)
                if i < NT:
                    s_chain(i)
                if 2 <= i <= NT + 1:
                    s_pv_emit(i - 2)
                if i + 2 < NT:
                    s_qk_emit(i + 2)
            P.op('sp', lambda e: e.dma_start(out=x32[0][0:1, 0:1], in_=cin['onesrow'][0:1, 0:1]),
                 waits=[os_tok[0], os_tok[1], fd_tok[0], fd_tok[1]], inc=s_phase)
            P.op('sp', lambda e: e.dma_start(out=x32[0][0:1, 1:2], in_=cin['onesrow'][0:1, 0:1]),
                 waits=[(s_phase.h, s_phase.v)], inc=s_phase)
            P.emit()
            P.release(_mk)
        p1_done = (s_phase.h, s_phase.v)
    if fused is not None:
        return mixT, p1_done
    return nc


def _p1_in_maps(x, even_norm, even_w_in, even_b_f, even_q_gain, even_k_gain):
    S = x.shape[1]
    consts = {'c_' + k: v for k, v in _consts_p1().items()}
    W = even_w_in[0]
    maps = []
    xTs = [np.ascontiguousarray(x[b].T) for b in range(x.shape[0])]
    gnl = np.ascontiguousarray(even_norm[0].reshape(8, 128).T)
    for c in range(8):
        b, h = c // 4, c % 4
        cols = np.concatenate([
            np.arange(h * 128, (h + 1) * 128),
            512 + np.arange(h * 128, (h + 1) * 128),
            1024 + np.arange(h * 128, (h + 1) * 128),
            1540 + np.arange(h * 128, (h + 1) * 128),
            2052 + np.arange(h * 128, (h + 1) * 128),
            2564 + np.arange(h * 128, (h + 1) * 128),
            3076 + np.arange(h * 128, (h + 1) * 128),
            3076 + 512 + np.arange(h * 128, (h + 1) * 128),
            np.array([1536 + h]),
        ])
        vec = np.zeros((128, 4), np.float32)
        vec[:, 0] = even_q_gain[0]
        vec[:, 1] = even_k_gain[0]
        vec[:, 2] = even_b_f[0, h]
        m = dict(xT=xTs[b], w=np.ascontiguousarray(W[:, cols]), gn=gnl, vec=vec)
        m.update(consts)
        maps.append(m)
    return maps


DIL = (1, 4, 16)


def _sigma(g, h):
    k = g * 8 + h
    return float(2.0 ** (-8.0 * (k + 1) / 24.0)) * DIL[g]


def _consts_p2():
    bf = ml_dtypes.bfloat16
    c = {}
    c['ident_bf'] = np.eye(128, dtype=np.float32).astype(bf)
    c['ones_bf'] = np.ones((128, 128), np.float32).astype(bf)
    bd = np.zeros((128, 128), np.float32)
    bd[0:64, 0:64] = 1.0
    bd[64:128, 64:128] = 1.0
    c['bd64'] = bd.astype(bf)
    s = np.arange(128)[:, None].astype(np.float32)
    a = np.arange(128)[None, :].astype(np.float32)
    bp = np.where(s >= a, s - 128 - a, NEG)
    bc = np.where(s <= a, s - a, NEG)
    c['bprev'] = np.tile(bp, (1, 4)).astype(bf)
    c['bcur'] = np.tile(bc, (1, 4)).astype(bf)
    e0 = np.zeros((128, 128), np.float32)
    e0[:, 0:64] = 1.0
    e1 = np.zeros((128, 128), np.float32)
    e1[:, 64:128] = 1.0
    c['e01'] = np.stack([e0, e1], axis=1).astype(bf)
    qc = np.zeros((128, 12), np.float32)
    for g in range(3):
        for hp in range(4):
            for d in range(128):
                qc[d, g * 4 + hp] = 0.125 / _sigma(g, 2 * hp + d // 64)
    c['qcst'] = qc
    return c


def build_p2(T=4096, H=2048, debug=False, fused=None):
    TT = T + H
    NCH = TT // 512
    NH = H // 512
    NSB = TT // 128
    pre = "b_" if fused is not None else ""
    nc = fused['nc'] if fused is not None else bass.Bass("TRN2", target_bir_lowering=False)

    def dt(name, *a_, **k_):
        return nc.dram_tensor(pre + name, *a_, **k_)
    if fused is not None:
        mT = fused['mT']
    else:
        mT = dt("mT", [1024, TT], BF16, kind="ExternalInput").ap()
    xT2 = dt("xT2", [1024, TT], F32, kind="ExternalInput").ap()
    wo0 = dt("wo0", [1024, 1024], F32, kind="ExternalInput").ap()
    wi1 = dt("wi1", [1024, 5120], F32, kind="ExternalInput").ap()
    wo1 = dt("wo1", [512, 1024], F32, kind="ExternalInput").ap()
    gn1 = dt("gn1", [128, 8], F32, kind="ExternalInput").ap()
    vec1 = dt("vec1", [128, 4], F32, kind="ExternalInput").ap()
    cn = _consts_p2()
    cin = {}
    for k, v in cn.items():
        mdt = {np.dtype('float32'): F32, np.dtype('float16'): F16}.get(v.dtype, BF16)
        cin[k] = dt("c_" + k, list(v.shape), mdt, kind="ExternalInput").ap()
    outT = dt("outT", [1024, T], F32, kind="ExternalOutput").ap()
    sk = "ExternalOutput" if debug else "Internal"
    scH = dt("scH", [128, 8, T], F32, kind=sk).ap()
    scQ = dt("scQ", [12, 128, T], BF16, kind=sk).ap()
    scK = dt("scK", [12, 128, TT], BF16, kind=sk).ap()
    scV = dt("scV", [12, 128, TT], BF16, kind=sk).ap()
    scG = dt("scG", [4, 128, T], F32, kind=sk).ap()
    scA = dt("scA", [4, 128, T], F32, kind=sk).ap()

    from contextlib import ExitStack
    with ExitStack() as top:
        def sb(name, shape, dtype):
            return top.enter_context(nc.sbuf_tensor(pre + name, shape, dtype))
        if fused is not None:
            P = fused['P']
        else:
            sems = [top.enter_context(nc.semaphore("s%d" % i)) for i in range(48)]
            P = Prog(nc, sems)
        t_start = fused['start_tok'] if fused is not None else None
        t_midx = None
        ident_bf = sb("ident_bf", [128, 128], BF16)
        ones_bf = sb("ones_bf", [128, 128], BF16)
        bd64 = sb("bd64", [128, 128], BF16)
        bprev = sb("bprev", [128, 512], BF16)
        bcur = sb("bcur", [128, 512], BF16)
        e01 = sb("e01", [128, 2, 128], BF16)
        e01h = sb("e01h", [128, 2, 128], BF16)
        qcst = sb("qcst", [128, 12], F32)
        qs = sb("qs", [128, 12], F32)
        gn_sb = sb("gn_sb", [128, 8], F32)
        vec_sb = sb("vec_sb", [128, 4], F32)
        s_const = P.dsem()
        t_const = None
        for name, t in (("ident_bf", ident_bf), ("ones_bf", ones_bf), ("bd64", bd64), ("bprev", bprev),
                        ("bcur", bcur), ("e01", e01), ("qcst", qcst)):
            t_const = P.op('sp', lambda e, t=t, name=name: e.dma_start(out=t[:], in_=cin[name]), waits=[t_start], inc=s_const)
        P.op('sp', lambda e: e.dma_start(out=gn_sb[:], in_=gn1), inc=s_const)
        t_const = P.op('sp', lambda e: e.dma_start(out=vec_sb[:], in_=vec1), inc=s_const)
        s_misc = P.sem()
        s_phase = P.dsem()
        P.op('dve', lambda e: e.tensor_scalar(qs[:], qcst[:], vec_sb[:, 0:1], None, ALU.mult), waits=[t_const], inc=s_misc)
        t_misc = P.op('dve', lambda e: e.tensor_scalar(e01h[:], e01[:], vec_sb[:, 2:3], None, ALU.mult), inc=s_misc)

        with ExitStack() as ph:
            _mk = P.mark()
            def sbp(name, shape, dtype):
                return ph.enter_context(nc.sbuf_tensor(pre + name, shape, dtype))

            def psp(name, shape, dtype):
                return ph.enter_context(nc.psum_tensor(pre + name, shape, dtype))
            Wo0 = sbp("Wo0", [128, 8, 1024], BF16)
            Wi = sbp("Wi", [128, 8, 5120], BF16)
            wst = [sbp("wst%d" % i, [128, 1024], F32) for i in range(2)]
            mc = [sbp("mc%d" % i, [128, 8, 512], BF16) for i in range(1)] * 2
            xc = [sbp("xc%d" % i, [128, 8, 512], F32) for i in range(1)] * 2
            h1f = sbp("h1f", [128, 8, 512], F32)
            h1b = [sbp("h1b%d" % i, [128, 8, 512], BF16) for i in range(1)] * 2
            sqb = sbp("sqb", [128, 8, 512], BF16)
            lnv = sbp("lnv", [128, 512], F32)
            rstd = [sbp("rstd%d" % i, [128, 512], F32) for i in range(2)]
            pq = sbp("pq", [128, 512], F32)
            sq2 = sbp("sq2", [128, 512], BF16)
            lnv2 = sbp("lnv2", [128, 512], F32)
            rstd2 = sbp("rstd2", [128, 512], F32)
            ob = [sbp("ob%d" % i, [128, 512], BF16) for i in range(4)]
            g_raw = sbp("graw", [128, 512], F32)
            g_e = sbp("ge", [128, 512], F32)
            g_o = [sbp("go%d" % i, [128, 512], F32) for i in range(2)]
            SS = psp("SS", [128, 512], F32)
            SS2 = psp("SS2", [128, 512], F32)
            PJ = [psp("PJ%d" % i, [128, 512], F32) for i in range(3)]

            s_wl = [P.dsem(), P.dsem()]
            s_wc = P.sem()
            wc_tok = [None, None]
            t_w = None
            wi = 0
            wo0_r = wo0.rearrange("(c p) n -> p c n", p=128)
            wi1_r = wi1.rearrange("(c p) n -> p c n", p=128)
            for c in range(8):
                b = wi % 2
                wi += 1
                tl = P.op('sp', lambda e, c=c, b=b: e.dma_start(out=wst[b][:], in_=wo0_r[:, c, :]), waits=[wc_tok[b], t_start], inc=s_wl[b])
                t_w = P.op('pool', lambda e, c=c, b=b: e.tensor_copy(Wo0[:, c, :], wst[b][:]), waits=[tl], inc=s_wc)
                wc_tok[b] = t_w
            for c in range(8):
                for q in range(5):
                    b = wi % 2
                    wi += 1
                    tl = P.op('sp', lambda e, c=c, b=b, q=q: e.dma_start(out=wst[b][:], in_=wi1_r[:, c, q * 1024:(q + 1) * 1024]),
                              waits=[wc_tok[b]], inc=s_wl[b])
                    t_w = P.op('pool', lambda e, c=c, b=b, q=q: e.tensor_scalar(Wi[:, c, q * 1024:(q + 1) * 1024], wst[b][:],
                                                                               gn_sb[:, c:c + 1], None, ALU.mult),
                               waits=[tl, t_const], inc=s_wc)
                    wc_tok[b] = t_w

            s_ml = [P.dsem()] * 2
            s_xl = [P.dsem()] * 2
            s_pj = P.sem()
            s_ev = P.sem()
            s_act = P.sem()
            s_dve = P.sem()
            s_pe2 = P.sem()
            s_cast = P.sem()
            s_ss = P.sem()
            s_hst = P.dsem()
            s_ost = [P.dsem() for _ in range(4)]
            s_gst = [P.dsem() for _ in range(2)]
            mT_r = mT.rearrange("(c p) t -> p c t", p=128)
            xT_r = xT2.rearrange("(c p) t -> p c t", p=128)
            m_free = _One()
            x_free = _One()
            h1b_free = _One()
            rstd_free = [None, None]
            pj_free = [None, None, None]
            pj_i = 0
            act_prev = None
            dve_prev = None
            ss_free = None
            ss2_free = None
            sqb_free = None
            h1f_free = []
            ob_tok = [None] * 4
            ob_i = 0
            go_tok = [None, None]
            go_i = 0
            hst_tok = None
            for n in range(NCH):
                b = n % 2
                cs = slice(n * 512, (n + 1) * 512)
                own = n >= NH
                ocs = slice((n - NH) * 512, (n - NH + 1) * 512)
                tm = P.op('sp', lambda e, b=b, cs=cs: e.dma_start(out=mc[b][:], in_=mT_r[:, :, cs]), waits=[m_free[b], t_start], inc=s_ml[b])
                tx = None
                for hlf in range(2):
                    tx = P.op('sp', lambda e, b=b, cs=cs, hlf=hlf: e.dma_start(out=xc[b][:, 4 * hlf:4 * hlf + 4, :],
                                                                              in_=xT_r[:, 4 * hlf:4 * hlf + 4, cs]),
                              waits=[x_free[b]], inc=s_xl[b])
                for o in range(8):
                    k = pj_i % 3
                    pj_i += 1
                    t = None
                    for c in range(8):
                        t = P.op('pe', lambda e, c=c, k=k, o=o, b=b: e.matmul(PJ[k][:], Wo0[:, c, o * 128:(o + 1) * 128], mc[b][:, c, :],
                                                                             start=(c == 0), stop=(c == 7)),
                                 waits=[tm, t_w, pj_free[k]] if c == 0 else (), inc=s_pj if c == 7 else None)
                    if o == 7:
                        m_free[b] = t
                    t = P.op('dve', lambda e, k=k, o=o, b=b: e.tensor_tensor(h1f[:, o, :], PJ[k][:], xc[b][:, o, :], ALU.add),
                             waits=[t, tx, dve_prev] + (h1f_free if o == 0 else []), inc=s_ev)
                    pj_free[k] = t
                    dve_prev = t
                x_free[b] = dve_prev
                th1 = dve_prev
                h1f_free = []
                if own:
                    hst_tok = P.op('pool', lambda e, ocs=ocs: e.dma_start(out=scH[:, :, ocs], in_=h1f[:]), waits=[th1], inc=s_hst)
                    h1f_free.append(hst_tok)
                tsq = P.op('act', lambda e: e.activation(sqb[:], h1f[:], AF.Square), waits=[th1, sqb_free, act_prev], inc=s_act)
                act_prev = tsq
                tcast = P.op('pool', lambda e, b=b: e.tensor_copy(h1b[b][:], h1f[:]), waits=[th1, h1b_free[b]], inc=s_cast)
                h1f_free += [tsq, tcast]
                for c in range(8):
                    t = P.op('pe', lambda e, c=c: e.matmul(SS[:], ones_bf[:], sqb[:, c, :], start=(c == 0), stop=(c == 7)),
                             waits=[tsq, ss_free] if c == 0 else (), inc=s_ss if c == 7 else None)
                sqb_free = t
                t = P.op('act', lambda e: e.activation(lnv[:], SS[:], AF.Ln, bias=EPS, scale=1.0 / 1024), waits=[t, act_prev], inc=s_act)
                ss_free = t
                rstd_tok = P.op('act', lambda e, b=b: e.activation(rstd[b][:], lnv[:], AF.Exp, scale=-0.5), waits=[t, rstd_free[b]], inc=s_act)
                act_prev = rstd_tok

                def proj(col0):
                    nonlocal pj_i
                    k = pj_i % 3
                    pj_i += 1
                    t = None
                    for c in range(8):
                        t = P.op('pe', lambda e, c=c, k=k, col0=col0, b=b: e.matmul(PJ[k][:], Wi[:, c, col0:col0 + 128], h1b[b][:, c, :],
                                                                                   start=(c == 0), stop=(c == 7)),
                                 waits=[tcast, pj_free[k]] if c == 0 else (), inc=s_pj if c == 7 else None)
                    return k, t

                def evac(k, fn, waits=()):
                    nonlocal dve_prev
                    t = P.op('dve', fn, waits=list(waits) + [rstd_tok, dve_prev], inc=s_ev)
                    pj_free[k] = t
                    dve_prev = t
                    return t

                last_pe = None
                for kind in ((0, 1, 2, 3) if own else (1, 2)):
                    nblk = 4 if kind == 3 else 12
                    for blk in range(nblk):
                        col0 = (0, 1536, 3072, 4608)[kind] + blk * 128
                        k, tp = proj(col0)
                        last_pe = tp
                        if kind in (0, 1):
                            tq = evac(k, lambda e, k=k, b=b: e.tensor_tensor(pq[:], PJ[k][:], rstd[b][:], ALU.mult), [tp])
                            t = P.op('act', lambda e: e.activation(sq2[:], pq[:], AF.Square), waits=[tq, act_prev], inc=s_act)
                            act_prev = t
                            t = P.op('pe', lambda e: e.matmul(SS2[:], bd64[:], sq2[:], start=True, stop=True), waits=[t, ss2_free], inc=s_pe2)
                            t = P.op('act', lambda e: e.activation(lnv2[:], SS2[:], AF.Ln, bias=EPS, scale=1.0 / 64), waits=[t, act_prev], inc=s_act)
                            ss2_free = t
                            t = P.op('act', lambda e: e.activation(rstd2[:], lnv2[:], AF.Exp, scale=-0.5), waits=[t], inc=s_act)
                            act_prev = t
                            gcol = qs[:, blk:blk + 1] if kind == 0 else vec_sb[:, 1:2]
                            oi = ob_i % 4
                            ob_i += 1
                            t = P.op('dve', lambda e, gcol=gcol, oi=oi: e.scalar_tensor_tensor(ob[oi][:], pq[:], gcol, rstd2[:], ALU.mult, ALU.mult),
                                     waits=[t, dve_prev, t_misc, ob_tok[oi]], inc=s_dve)
                            dve_prev = t
                            if kind == 0:
                                ob_tok[oi] = P.op('pool', lambda e, oi=oi, blk=blk, ocs=ocs: e.dma_start(out=scQ[blk, :, ocs], in_=ob[oi][:]),
                                                  waits=[t], inc=s_ost[oi])
                            else:
                                ob_tok[oi] = P.op('pool', lambda e, oi=oi, blk=blk, cs=cs: e.dma_start(out=scK[blk, :, cs], in_=ob[oi][:]),
                                                  waits=[t], inc=s_ost[oi])
                        elif kind == 2:
                            oi = ob_i % 4
                            ob_i += 1
                            t = evac(k, lambda e, k=k, b=b, oi=oi: e.tensor_tensor(ob[oi][:], PJ[k][:], rstd[b][:], ALU.mult), [tp, ob_tok[oi]])
                            ob_tok[oi] = P.op('pool', lambda e, oi=oi, blk=blk, cs=cs: e.dma_start(out=scV[blk, :, cs], in_=ob[oi][:]),
                                              waits=[t], inc=s_ost[oi])
                        else:
                            tg = evac(k, lambda e, k=k, b=b: e.tensor_tensor(g_raw[:], PJ[k][:], rstd[b][:], ALU.mult), [tp])
                            t = P.op('act', lambda e: e.activation(g_e[:], g_raw[:], AF.Exp, scale=-1.0), waits=[tg, act_prev], inc=s_act)
                            act_prev = t
                            t = P.op('dve', lambda e: e.tensor_scalar(g_e[:], g_e[:], 1.0, None, ALU.add), waits=[t, dve_prev], inc=s_dve)
                            t = P.op('dve', lambda e: e.reciprocal(g_e[:], g_e[:]), waits=[t], inc=s_dve)
                            gi = go_i % 2
                            go_i += 1
                            t = P.op('dve', lambda e, gi=gi: e.tensor_tensor(g_o[gi][:], g_raw[:], g_e[:], ALU.mult), waits=[t, go_tok[gi]], inc=s_dve)
                            dve_prev = t
                            go_tok[gi] = P.op('sp', lambda e, gi=gi, blk=blk, ocs=ocs: e.dma_start(out=scG[blk, :, ocs], in_=g_o[gi][:]),
                                              waits=[t], inc=s_gst[gi])
                rstd_free[b] = dve_prev
                h1b_free[b] = last_pe
            fin = [t for t in ob_tok + go_tok + [hst_tok, dve_prev] if t is not None]
            P.op('sp', lambda e: e.dma_start(out=lnv[0:1, 0:1], in_=cin['qcst'][0:1, 0:1]), waits=fin, inc=s_phase)
            P.emit()
            P.release(_mk)

        tp_phase = (s_phase.h, s_phase.v)
        with ExitStack() as ph:
            _mk = P.mark()
            def sbp(name, shape, dtype):
                return ph.enter_context(nc.sbuf_tensor(pre + name, shape, dtype))

            def psp(name, shape, dtype):
                return ph.enter_context(nc.psum_tensor(pre + name, shape, dtype))
            QT = [sbp("QT%d" % i, [128, T], BF16) for i in range(1)] * 2
            KT = [sbp("KT%d" % i, [128, TT], BF16) for i in range(1)] * 2
            VT = [sbp("VT%d" % i, [128, TT], BF16) for i in range(1)] * 2
            Vpad = sbp("Vpad", [128, NSB, 2, 128], BF16)
            accO = sbp("accO", [128, T], F32)
            accD = sbp("accD", [128, T], F32)
            mix1 = [sbp("mix1_%d" % i, [128, T], BF16) for i in range(4)]
            Pt = [sbp("Pt%d" % i, [128, 512], BF16) for i in range(4)]
            gt = [sbp("gt%d" % i, [128, 512], F32) for i in range(2)]
            rd = sbp("rd", [128, 512], F32)
            att = sbp("att", [128, 512], F32)
            Wo1 = sbp("Wo1", [128, 4, 1024], BF16)
            wst1 = sbp("wo1stage", [128, 4, 1024], F32)
            hres = [sbp("hres%d" % i, [128, 8, 512], F32) for i in range(2)]
            oo = [sbp("oo%d" % i, [128, 512], F32) for i in range(2)]
            Sk = [psp("Sk%d" % i, [128, 512], F32) for i in range(4)]
            Ob = psp("Ob", [128, 512], F32)
            Db = psp("Db", [128, 512], F32)
            PO = psp("PO", [128, 512], F32)
            TPb = psp("TPb", [128, 4, 128], BF16)
            s_ld = [P.dsem()] * 2
            s_pe = P.sem()
            s_act = P.sem()
            s_dve = P.sem()
            s_pl = P.sem()
            s_gl = [P.dsem(), P.dsem()]
            s_hl = [P.dsem(), P.dsem()]
            s_os = [P.dsem(), P.dsem()]
            s_w1 = P.dsem()
            s_as = P.dsem()
            t_vz = P.op('pool', lambda e: e.memset(Vpad[:], 0.0), waits=[tp_phase], inc=s_pl)
            tw = P.op('sp', lambda e: e.dma_start(out=wst1[:], in_=wo1.rearrange("(c p) n -> p c n", p=128)), waits=[tp_phase], inc=s_w1)
            t_wo1 = P.op('pool', lambda e: e.tensor_copy(Wo1[:], wst1[:]), waits=[tw], inc=s_pl)
            passes = [(hp, g) for hp in range(4) for g in range(3)]
            ld_tok = _One()
            buf_free = _One()

            def load(pi):
                hp, g = passes[pi]
                blk = g * 4 + hp
                b = pi % 2
                P.op('sp', lambda e: e.dma_start(out=QT[b][:], in_=scQ[blk]), waits=[tp_phase, buf_free[b]], inc=s_ld[b])
                P.op('sp', lambda e: e.dma_start(out=KT[b][:], in_=scK[blk]), inc=s_ld[b])
                ld_tok[b] = P.op('sp', lambda e: e.dma_start(out=VT[b][:], in_=scV[blk]), inc=s_ld[b])

            load(0)
            pe_prev = None
            act_prev = None
            dve_prev = None
            p_free = [None] * 4
            sk_free = [None] * 4
            od_free = None
            tp_free = None
            vpad_free = None
            acc_free = None
            gl_tok = [None, None]
            gt_free = [None, None]
            as_tok = None
            for pi, (hp, g) in enumerate(passes):
                b = pi % 2
                d = DIL[g]
                tl = ld_tok[b]
                span = 128 * d
                nspan = TT // span
                hspan = H // span

                def tok_ap(tensor, u, r, off=0):
                    s0 = u * span + r - off
                    return tensor[:, s0:s0 + span - d + 1:d] if d > 1 else tensor[:, s0:s0 + 128]

                sbl = [(u, r) for u in range(nspan) for r in range(d)]
                for q0 in range(0, len(sbl), 4):
                    t = None
                    for qi in range(4):
                        u, r = sbl[q0 + qi]
                        s0 = u * span + r
                        src = VT[b][:, s0:s0 + span - d + 1:d] if d > 1 else VT[b][:, s0:s0 + 128]
                        t = P.op('pe', lambda e, qi=qi, src=src: e.transpose(TPb[:, qi, :], src, ident_bf[:]),
                                 waits=[tl, tp_free, t_const] if qi == 0 else (), inc=s_pe if qi == 3 else None)
                    halo = (sbl[q0][0] < hspan)
                    for hh in range(2):
                        dst = Vpad[:, q0:q0 + 4, hh, hh * 64:(hh + 1) * 64]
                        srcp = TPb[:, :, hh * 64:(hh + 1) * 64]
                        if halo:
                            t2 = P.op('dve', lambda e, dst=dst, srcp=srcp: e.tensor_scalar(dst, srcp, vec_sb[:, 2:3], None, ALU.mult),
                                      waits=[t, dve_prev, vpad_free, t_vz, t_const], inc=s_dve)
                        else:
                            t2 = P.op('dve', lambda e, dst=dst, srcp=srcp: e.tensor_copy(dst, srcp),
                                      waits=[t, dve_prev, vpad_free, t_vz], inc=s_dve)
                        dve_prev = t2
                    tp_free = dve_prev
                t_v = dve_prev
                qsb = [(u, r) for u in range(hspan, nspan) for r in range(d)]
                if d == 1:
                    batches = [qsb[i:i + 4] for i in range(0, len(qsb), 4)]
                else:
                    batches = [[(u, r0 + i) for i in range(4)] for u in range(hspan, nspan) for r0 in range(0, d, 4)]
                for bt in batches:
                    toks_s = {}
                    for hh in range(2):
                        hs = slice(hh * 64, (hh + 1) * 64)
                        for kt in range(2):
                            bank = Sk[2 * hh + kt]
                            for qi, (u, r) in enumerate(bt):
                                ku = u - 1 + kt
                                s0 = ku * span + r
                                kap = KT[b][hs, s0:s0 + span - d + 1:d] if d > 1 else KT[b][hs, s0:s0 + 128]
                                q0_ = u * span + r - H
                                qap = QT[b][hs, q0_:q0_ + span - d + 1:d] if d > 1 else QT[b][hs, q0_:q0_ + 128]
                                P.op('pe', lambda e, bank=bank, kap=kap, qap=qap, qi=qi: e.matmul(
                                    bank[:, qi * 128:(qi + 1) * 128], kap, qap, start=(qi == 0), stop=False, skip_group_check=True),
                                    waits=[tl, sk_free[2 * hh + kt]] if qi == 0 else ())
                            bt_tile = bprev if kt == 0 else bcur
                            toks_s[(hh, kt)] = P.op('pe', lambda e, bank=bank, bt_tile=bt_tile: e.matmul(
                                bank[:], ident_bf[:], bt_tile[:], start=False, stop=True, skip_group_check=True), inc=s_pe)
                    toks_p = {}
                    for hh in range(2):
                        sg = _sigma(g, 2 * hp + hh)
                        for kt in range(2):
                            i4 = 2 * hh + kt
                            t = P.op('act', lambda e, i4=i4, sg=sg: e.activation(Pt[i4][:], Sk[i4][:], AF.Exp, scale=sg),
                                     waits=[toks_s[(hh, kt)], p_free[i4], act_prev], inc=s_act)
                            act_prev = t
                            sk_free[i4] = t
                            toks_p[(hh, kt)] = t
                    first = True
                    t = None
                    for qi, (u, r) in enumerate(bt):
                        for bank, den in ((Ob, False), (Db, True)):
                            k4 = 0
                            for hh in range(2):
                                for kt in range(2):
                                    ku = u - 1 + kt
                                    sbi = ku * d + r
                                    if den:
                                        lt = (e01h if ku < hspan else e01)[:, hh, :]
                                    else:
                                        lt = Vpad[:, sbi, hh, :]
                                    i4 = 2 * hh + kt
                                    t = P.op('pe', lambda e, bank=bank, lt=lt, i4=i4, qi=qi, st=(qi == 0 and k4 == 0), sp_=(k4 == 3): e.matmul(
                                        bank[:, qi * 128:(qi + 1) * 128], lt, Pt[i4][:, qi * 128:(qi + 1) * 128], start=st, stop=sp_,
                                        skip_group_check=True),
                                        waits=[toks_p[(hh, kt)], od_free, t_v, t_misc] if (qi == 0 and not den and k4 == 0) else
                                              ([toks_p[(hh, kt)]] if qi == 0 else ()),
                                        inc=s_pe if (qi == 3 and den and k4 == 3) else None)
                                    k4 += 1
                    t_pv = t
                    for i4 in range(4):
                        p_free[i4] = t_pv
                    u0, r0 = bt[0]
                    if d == 1:
                        c0 = u0 * 128 - H
                        aO = accO[:, c0:c0 + 512]
                        aD = accD[:, c0:c0 + 512]
                        pO = Ob[:]
                        pD = Db[:]
                    else:
                        c0 = u0 * span - H
                        aO = accO[:, c0:c0 + span].rearrange("p (i d) -> p d i", d=d)[:, r0:r0 + 4, :]
                        aD = accD[:, c0:c0 + span].rearrange("p (i d) -> p d i", d=d)[:, r0:r0 + 4, :]
                        pO = Ob[:].rearrange("p (q i) -> p q i", q=4)
                        pD = Db[:].rearrange("p (q i) -> p q i", q=4)
                    if g == 0:
                        t = P.op('dve', lambda e, aO=aO, pO=pO: e.tensor_copy(aO, pO), waits=[t_pv, dve_prev, acc_free], inc=s_dve)
                        t = P.op('dve', lambda e, aD=aD, pD=pD: e.tensor_copy(aD, pD), waits=[t], inc=s_dve)
                    else:
                        t = P.op('dve', lambda e, aO=aO, pO=pO: e.tensor_tensor(aO, pO, aO, ALU.add), waits=[t_pv, dve_prev], inc=s_dve)
                        t = P.op('dve', lambda e, aD=aD, pD=pD: e.tensor_tensor(aD, pD, aD, ALU.add), waits=[t], inc=s_dve)
                    dve_prev = t
                    od_free = t
                buf_free[b] = t_pv
                vpad_free = t_pv
                if pi + 1 < len(passes):
                    load(pi + 1)
                if g == 2:
                    for ch in range(T // 512):
                        ocs = slice(ch * 512, (ch + 1) * 512)
                        gi = ch % 2
                        gl_tok[gi] = P.op('sp', lambda e, gi=gi, ocs=ocs, hp=hp: e.dma_start(out=gt[gi][:], in_=scG[hp, :, ocs]),
                                          waits=[gt_free[gi]], inc=s_gl[gi])
                        t = P.op('dve', lambda e, ocs=ocs: e.reciprocal(rd[:], accD[:, ocs]), waits=[dve_prev, as_tok], inc=s_dve)
                        t = P.op('dve', lambda e, ocs=ocs: e.tensor_tensor(att[:], accO[:, ocs], rd[:], ALU.mult), waits=[t], inc=s_dve)
                        if debug:
                            as_tok = P.op('pool', lambda e, ocs=ocs, hp=hp: e.dma_start(out=scA[hp, :, ocs], in_=att[:]), waits=[t], inc=s_as)
                        t = P.op('dve', lambda e, gi=gi, ocs=ocs, hp=hp: e.tensor_tensor(mix1[hp][:, ocs], att[:], gt[gi][:], ALU.mult),
                                 waits=[t, gl_tok[gi]], inc=s_dve)
                        gt_free[gi] = t
                        dve_prev = t
                    acc_free = dve_prev
            t_mix = dve_prev
            hl_tok = [None, None]
            h_free = [None, None]
            os_tok = [None, None]
            po_free = None
            oi = 0
            for ch in range(T // 512):
                ocs = slice(ch * 512, (ch + 1) * 512)
                hb = ch % 2
                hl_tok[hb] = P.op('sp', lambda e, hb=hb, ocs=ocs: e.dma_start(out=hres[hb][:], in_=scH[:, :, ocs]),
                                  waits=[h_free[hb], tp_phase], inc=s_hl[hb])
                for o in range(8):
                    t = None
                    for c in range(4):
                        t = P.op('pe', lambda e, c=c, o=o, ocs=ocs: e.matmul(PO[:], Wo1[:, c, o * 128:(o + 1) * 128], mix1[c][:, ocs],
                                                                          start=(c == 0), stop=(c == 3)),
                                 waits=[t_mix, t_wo1, po_free] if c == 0 else (), inc=s_pe if c == 3 else None)
                    ob_ = oi % 2
                    oi += 1
                    t = P.op('dve', lambda e, ob_=ob_, hb=hb, o=o: e.tensor_tensor(oo[ob_][:], PO[:], hres[hb][:, o, :], ALU.add),
                             waits=[t, hl_tok[hb], dve_prev, os_tok[ob_]], inc=s_dve)
                    dve_prev = t
                    po_free = t
                    os_tok[ob_] = P.op('sp', lambda e, ob_=ob_, o=o, ocs=ocs: e.dma_start(out=outT[o * 128:(o + 1) * 128, ocs], in_=oo[ob_][:]),
                                       waits=[t], inc=s_os[ob_])
                h_free[hb] = dve_prev
            P.op('sp', lambda e: e.dma_start(out=rd[0:1, 0:1], in_=cin['qcst'][0:1, 0:1]),
                 waits=[os_tok[0], os_tok[1]] + ([as_tok] if as_tok else []), inc=s_phase)
            P.op('sp', lambda e: e.dma_start(out=rd[0:1, 1:2], in_=cin['qcst'][0:1, 0:1]), waits=[(s_phase.h, s_phase.v)], inc=s_phase)
            P.emit()
            P.release(_mk)
    if fused is not None:
        return outT
    return nc


def _p2_in_maps(x, mix_all, even_w_out, odd_norm, odd_w_in, odd_q_gain, odd_k_gain, odd_w_out, T=4096, H=2048):
    B, S, _ = x.shape
    bf = ml_dtypes.bfloat16
    consts = {'c_' + k: v for k, v in _consts_p2().items()}
    gnl = np.ascontiguousarray(odd_norm[0].reshape(8, 128).T)
    maps = []
    nj = S // T
    for b in range(B):
        mixT = np.zeros((1024, H + S), dtype=bf)
        for h in range(4):
            m = mix_all[b * 4 + h]
            mixT[h * 128:(h + 1) * 128, H:] = m[0:128]
            mixT[512 + h * 128:512 + (h + 1) * 128, H:] = m[128:256]
        xT = np.zeros((1024, H + S), dtype=np.float32)
        xT[:, H:] = x[b].T
        for j in range(nj):
            vec = np.zeros((128, 4), np.float32)
            vec[:, 0] = np.tile(odd_q_gain[0], 2)
            vec[:, 1] = np.tile(odd_k_gain[0], 2)
            vec[:, 2] = 0.0 if j == 0 else 1.0
            m = dict(mT=np.ascontiguousarray(mixT[:, j * T:j * T + T + H]),
                     xT2=np.ascontiguousarray(xT[:, j * T:j * T + T + H]),
                     wo0=np.ascontiguousarray(even_w_out[0]), wi1=np.ascontiguousarray(odd_w_in[0]),
                     wo1=np.ascontiguousarray(odd_w_out[0]), gn1=gnl, vec1=vec)
            m.update(consts)
            maps.append(m)
    return maps


_NC_CACHE = {}


def build_fused(S=16384, T=4096, H=2048):
    from contextlib import ExitStack
    nc = bass.Bass("TRN2", target_bir_lowering=False)
    with ExitStack() as st:
        sems = [st.enter_context(nc.semaphore("s%d" % i)) for i in range(48)]
        P = Prog(nc, sems)
        fused = dict(nc=nc, P=P)
        mixT, p1_done = build_p1(S, fused=fused)
        NQ = S // 2048
        gath = nc.dram_tensor("gath", [NQ, 1024, 2048], BF16).ap()
        cc = P.sem()
        tok = None
        for q in range(NQ):
            tok = P.op('pool', lambda e, q=q: e.collective_compute(
                "AllGather", ALU.bypass, replica_groups=[[0, 1, 2, 3], [4, 5, 6, 7]],
                ins=[mixT[q].opt()], outs=[gath[q].opt()]), waits=[p1_done], inc=cc)
        P.emit()
        mTloc = nc.dram_tensor("mTloc", [1024, T + H], BF16).ap()
        xidx_d = nc.dram_tensor("xidx", [128, 24], mybir.dt.int32, kind="ExternalInput").ap()
        with ExitStack() as ex:
            stg = [ex.enter_context(nc.sbuf_tensor("xstg%d" % i, [128, 2048], BF16)) for i in range(2)]
            xidx = ex.enter_context(nc.sbuf_tensor("xidx_sb", [128, 24], mybir.dt.int32))
            _mk = P.mark()
            s_i = P.dsem()
            s_g = [P.dsem(), P.dsem()]
            s_s = [P.dsem(), P.dsem()]
            t_i = P.op('sp', lambda e: e.dma_start(out=xidx[:], in_=xidx_d), waits=[tok], inc=s_i)
            tab = gath.rearrange("q ch w -> (q ch) w")
            st_tok = [None, None]
            i = 0
            for k in range(3):
                for c in range(8):
                    b = i % 2
                    tg = P.op('pool', lambda e, b=b, i=i: e.indirect_dma_start(
                        out=stg[b][:], out_offset=None, in_=tab,
                        in_offset=bass.IndirectOffsetOnAxis(ap=xidx[:, i:i + 1], axis=0),
                        bounds_check=NQ * 1024 - 1, oob_is_err=False), waits=[t_i, tok, st_tok[b]], inc=s_g[b])
                    st_tok[b] = P.op('sp', lambda e, b=b, k=k, c=c: e.dma_start(
                        out=mTloc[c * 128:(c + 1) * 128, k * 2048:(k + 1) * 2048], in_=stg[b][:]), waits=[tg], inc=s_s[b])
                    i += 1
            s_x = P.dsem()
            t_x = P.op('sp', lambda e: e.dma_start(out=xidx[0:1, 0:1], in_=xidx_d[0:1, 0:1]), waits=[st_tok[0], st_tok[1]], inc=s_x)
            P.emit()
            P.live.remove(s_x)
            P.release(_mk)
        fused['mT'] = mTloc
        fused['start_tok'] = t_x
        build_p2(T, H, fused=fused)
    return nc


def _fused_in_maps(x, even_norm, even_w_in, even_b_f, even_q_gain, even_k_gain, even_w_out,
                   odd_norm, odd_w_in, odd_q_gain, odd_k_gain, odd_w_out, T=4096, H=2048):
    B, S, _ = x.shape
    maps1 = _p1_in_maps(x, even_norm, even_w_in, even_b_f, even_q_gain, even_k_gain)
    consts = {'b_c_' + k: v for k, v in _consts_p2().items()}
    gnl = np.ascontiguousarray(odd_norm[0].reshape(8, 128).T)
    perm = np.concatenate([kk * 512 + r * 128 + np.arange(128) for r in range(4) for kk in range(2)])
    wo0p = np.ascontiguousarray(even_w_out[0][perm])
    wi1 = np.ascontiguousarray(odd_w_in[0])
    wo1 = np.ascontiguousarray(odd_w_out[0])
    nj = S // T
    maps = []
    for c in range(8):
        b, j = c // nj, c % nj
        m = {'a_' + k: v for k, v in maps1[c].items()}
        xT = np.zeros((1024, T + H), np.float32)
        lo = j * T - H
        if lo < 0:
            xT[:, H:] = x[b, 0:T].T
        else:
            xT[:, :] = x[b, lo:lo + T + H].T
        vec = np.zeros((128, 4), np.float32)
        vec[:, 0] = np.tile(odd_q_gain[0], 2)
        vec[:, 1] = np.tile(odd_k_gain[0], 2)
        vec[:, 2] = 0.0 if j == 0 else 1.0
        xidx = np.zeros((128, 24), np.int32)
        for k in range(3):
            q = max(2 * j - 1 + k, 0)
            for c8 in range(8):
                xidx[:, k * 8 + c8] = q * 1024 + c8 * 128 + np.arange(128)
        m.update({'b_xT2': xT, 'b_wo0': wo0p, 'b_wi1': wi1, 'b_wo1': wo1, 'b_gn1': gnl, 'b_vec1': vec, 'xidx': xidx})
        m.update(consts)
        maps.append(m)
    return maps


def kernel(x, even_norm, even_w_in, even_b_f, even_q_gain, even_k_gain, even_w_out,
           odd_norm, odd_w_in, odd_q_gain, odd_k_gain, odd_w_out):
    x = np.asarray(x, dtype=np.float32)
    args = [np.asarray(a, dtype=np.float32) for a in (even_norm, even_w_in, even_b_f, even_q_gain, even_k_gain, even_w_out,
                                                      odd_norm, odd_w_in, odd_q_gain, odd_k_gain, odd_w_out)]
    B, S, _ = x.shape
    T, H = 4096, 2048
    if 'fused' not in _NC_CACHE:
        _NC_CACHE['fused'] = build_fused(S, T, H)
    maps = _fused_in_maps(x, *args, T=T, H=H)
    res = run_bass_kernel_spmd(_NC_CACHE['fused'], maps, core_ids=list(range(8)))
    out = np.empty((B, S, 1024), np.float32)
    nj = S // T
    for c in range(8):
        b, j = c // nj, c % nj
        out[b, j * T:(j + 1) * T, :] = np.asarray(res.results[c]['b_outT']).T
    return out
```
